# Optimizing a Trainium2 kernel written in Bass

```python
import jax, jax.numpy as jnp
from jax import lax
import numpy as np

D_MODEL = 2048
BATCH = 2
SEQ = 8192
DEPTH = 2

MIX_WIDTH = 2 * D_MODEL
A_WIDTH = D_MODEL // 1 if False else MIX_WIDTH // 2
A_GROUPS = 8
A_CHUNK = 128
B_WIDTH = MIX_WIDTH // 2
B_HEAD_DIM = 64
B_HEADS = B_WIDTH // B_HEAD_DIM
B_GROUPS = 8
B_STATE = 128
B_CONV = 4
B_CHUNK = 128
B_XBC = B_WIDTH + 2 * B_GROUPS * B_STATE
C_WIDTH = MIX_WIDTH // 2
C_CONV = 3
D_WIDTH = MIX_WIDTH // 2
D_HEAD_DIM = 128
D_HEADS = D_WIDTH // D_HEAD_DIM
D_PATTERNS = ((128, 1), (512, 4), (2048, 16))

EPS = 1e-5
N_EVEN = (DEPTH + 1) // 2
N_ODD = DEPTH // 2
IN_EVEN = 3 * A_WIDTH + B_WIDTH + B_XBC + B_HEADS
IN_ODD = 4 * C_WIDTH + 4 * D_WIDTH

kernel_name = "hybrid_gmlp_ssd_shortconv_dilated_attn"


def rms_norm(x, g):
    xf = x.astype(jnp.float32)
    y = xf * lax.rsqrt(jnp.mean(xf * xf, axis=-1, keepdims=True) + EPS)
    return (y * g.astype(jnp.float32)).astype(x.dtype)


def layer_norm(x, g, b):
    xf = x.astype(jnp.float32)
    mu = jnp.mean(xf, axis=-1, keepdims=True)
    xc = xf - mu
    y = xc * lax.rsqrt(jnp.mean(xc * xc, axis=-1, keepdims=True) + EPS)
    return (y * g.astype(jnp.float32) + b.astype(jnp.float32)).astype(x.dtype)


def causal_dwconv(x, w):
    K, C = w.shape
    return lax.conv_general_dilated(
        x, w[:, None, :].astype(x.dtype), window_strides=(1,),
        padding=[(K - 1, 0)], dimension_numbers=("NWC", "WIO", "NWC"),
        feature_group_count=C)


def gmlp_branch(h, ln_g, ln_b, ws, bs):
    Bb, S, _ = h.shape
    u, v, z = jnp.split(h, 3, axis=-1)
    v = layer_norm(v, ln_g, ln_b)
    G, Q, _ = ws.shape
    causal = jnp.tril(jnp.ones((Q, Q), dtype=bool))
    ws_c = jnp.where(causal, ws, jnp.zeros_like(ws))
    vc = v.reshape(Bb, S // Q, Q, G, A_WIDTH // G)
    mixed = jnp.einsum("gts,bcsgd->bctgd", ws_c, vc) + bs.T[None, None, :, :, None]
    return jax.nn.silu(z) * (u * mixed.reshape(Bb, S, A_WIDTH))


def segsum(x):
    T = x.shape[-1]
    cs = jnp.cumsum(x, axis=-1)
    seg = cs[..., :, None] - cs[..., None, :]
    return jnp.where(jnp.tril(jnp.ones((T, T), dtype=bool)), seg, -jnp.inf)


def ssd_scan(x, dt, a, bm, cm):
    Bb, S, H, P = x.shape
    G, N = bm.shape[2], bm.shape[3]
    R = H // G
    Q = B_CHUNK
    nc = S // Q
    xdt = (x * dt[..., None]).reshape(Bb, nc, Q, G, R, P)
    adt = (dt * a).reshape(Bb, nc, Q, G, R).transpose(0, 3, 4, 1, 2)
    bc = bm.reshape(Bb, nc, Q, G, N)
    cc = cm.reshape(Bb, nc, Q, G, N)
    a_cs = jnp.cumsum(adt, axis=-1)
    L = jnp.exp(segsum(adt))
    cb = jnp.einsum("bclgn,bcsgn->bcgls", cc, bc)
    y_diag = jnp.einsum("bcgls,bgrcls,bcsgrp->bclgrp", cb, L, xdt)
    decay_states = jnp.exp(a_cs[..., -1:] - a_cs)
    states = jnp.einsum("bclgn,bgrcl,bclgrp->bcgrpn", bc, decay_states, xdt)
    chunk_decay = jnp.exp(a_cs[..., -1])

    def step(hstate, inp):
        s_c, d_c = inp
        return hstate * d_c[..., None, None] + s_c, hstate

    _, prev = lax.scan(step, jnp.zeros_like(states[:, 0]),
                       (jnp.moveaxis(states, 1, 0), jnp.moveaxis(chunk_decay, -1, 0)))
    prev = jnp.moveaxis(prev, 0, 1)
    y_off = jnp.einsum("bclgn,bcgrpn,bgrcl->bclgrp", cc, prev, jnp.exp(a_cs))
    return (y_diag + y_off).reshape(Bb, S, H, P)


def ssd_branch(h, conv_w, conv_b, dt_bias, a_log, d_skip, norm_g):
    Bb, S, _ = h.shape
    z = h[..., :B_WIDTH]
    xbc = h[..., B_WIDTH:B_WIDTH + B_XBC]
    dt_raw = h[..., B_WIDTH + B_XBC:]
    xbc = jax.nn.silu(causal_dwconv(xbc, conv_w) + conv_b.astype(xbc.dtype))
    gn = B_GROUPS * B_STATE
    xs = xbc[..., :B_WIDTH].astype(jnp.float32).reshape(Bb, S, B_HEADS, B_HEAD_DIM)
    bm = xbc[..., B_WIDTH:B_WIDTH + gn].astype(jnp.float32).reshape(Bb, S, B_GROUPS, B_STATE)
    cm = xbc[..., B_WIDTH + gn:].astype(jnp.float32).reshape(Bb, S, B_GROUPS, B_STATE)
    dt = jax.nn.softplus(dt_raw.astype(jnp.float32) + dt_bias.astype(jnp.float32))
    a = -jnp.exp(a_log.astype(jnp.float32))
    y = ssd_scan(xs, dt, a, bm, cm) + d_skip.astype(jnp.float32)[:, None] * xs
    y = y.reshape(Bb, S, B_WIDTH) * jax.nn.silu(z.astype(jnp.float32))
    yg = y.reshape(Bb, S, B_GROUPS, B_WIDTH // B_GROUPS)
    yg = yg * lax.rsqrt(jnp.mean(yg * yg, axis=-1, keepdims=True) + EPS)
    return (yg.reshape(Bb, S, B_WIDTH) * norm_g.astype(jnp.float32)).astype(h.dtype)


def shortconv_branch(h, conv_w):
    bg, cg, hx, z = jnp.split(h, 4, axis=-1)
    return jax.nn.silu(z) * (bg * causal_dwconv(cg * hx, conv_w))


def dilated_window_attention(q, k, v, dil, n_back):
    Bb, S, H, E = q.shape
    M = S // dil
    nb = -(-M // n_back)
    Mp = nb * n_back

    def to_blocks(t):
        t = t.reshape(Bb, M, dil, H, E)
        t = jnp.pad(t, ((0, 0), (0, Mp - M), (0, 0), (0, 0), (0, 0)))
        return t.reshape(Bb, nb, n_back, dil, H, E)

    def with_prev(t):
        prev = jnp.pad(t, ((0, 0), (1, 0), (0, 0), (0, 0), (0, 0), (0, 0)))[:, :-1]
        return jnp.concatenate([prev, t], axis=2)

    qb = to_blocks(q)
    kw = with_prev(to_blocks(k))
    vw = with_prev(to_blocks(v))
    s = jnp.einsum("bnarhe,bnjrhe->bnrhaj", qb, kw,
                   preferred_element_type=jnp.float32) * (E ** -0.5)
    a_idx = jnp.arange(n_back)[:, None]
    j_idx = jnp.arange(2 * n_back)[None, :]
    band = (j_idx >= a_idx) & (j_idx <= a_idx + n_back)
    key_ok = (jnp.arange(nb)[:, None] > 0) | (j_idx >= n_back)
    mask = (band[None] & key_ok[:, None, :])[None, :, None, None]
    s = jnp.where(mask, s, -jnp.inf)
    mx = jnp.max(s, axis=-1, keepdims=True)
    p = jnp.exp(s - mx)
    l = jnp.sum(p, axis=-1, keepdims=True)
    o = jnp.einsum("bnrhaj,bnjrhe->bnarhe", p / l, vw.astype(jnp.float32))
    lse = (mx + jnp.log(l))[..., 0].transpose(0, 1, 4, 2, 3)
    o = o.reshape(Bb, Mp, dil, H, E)[:, :M].reshape(Bb, S, H, E)
    lse = lse.reshape(Bb, Mp, dil, H)[:, :M].reshape(Bb, S, H)
    return o, lse


def dilated_attention_branch(h):
    Bb, S, _ = h.shape
    q, k, v, z = jnp.split(h, 4, axis=-1)
    q = q.reshape(Bb, S, D_HEADS, D_HEAD_DIM)
    k = k.reshape(Bb, S, D_HEADS, D_HEAD_DIM)
    v = v.reshape(Bb, S, D_HEADS, D_HEAD_DIM)
    outs, lses = [], []
    for window, dil in D_PATTERNS:
        o, lse = dilated_window_attention(q, k, v, dil, window // dil)
        outs.append(o)
        lses.append(lse)
    wts = jax.nn.softmax(jnp.stack(lses, axis=0), axis=0)
    o = jnp.sum(wts[..., None] * jnp.stack(outs, axis=0), axis=0)
    return jax.nn.silu(z) * o.reshape(Bb, S, D_WIDTH).astype(h.dtype)


def even_layer(x, norm_g, w_in, ln_g, ln_b, ws, bs, conv_w, conv_b, dt_bias,
               a_log, d_skip, ssd_norm_g, w_out):
    h = jnp.einsum("bsd,df->bsf", rms_norm(x, norm_g), w_in)
    ya = gmlp_branch(h[..., :3 * A_WIDTH], ln_g, ln_b, ws, bs)
    yb = ssd_branch(h[..., 3 * A_WIDTH:], conv_w, conv_b, dt_bias, a_log, d_skip, ssd_norm_g)
    y = jnp.concatenate([ya, yb.astype(ya.dtype)], axis=-1)
    return x + jnp.einsum("bsf,fd->bsd", y, w_out).astype(x.dtype)


def odd_layer(x, norm_g, w_in, sconv_w, w_out):
    h = jnp.einsum("bsd,df->bsf", rms_norm(x, norm_g), w_in)
    yc = shortconv_branch(h[..., :4 * C_WIDTH], sconv_w)
    yd = dilated_attention_branch(h[..., 4 * C_WIDTH:])
    y = jnp.concatenate([yc, yd.astype(yc.dtype)], axis=-1)
    return x + jnp.einsum("bsf,fd->bsd", y, w_out).astype(x.dtype)


def setup_inputs(seed: int = 0) -> dict:
    key = jax.random.key(seed)
    ks = jax.random.split(key, 20)
    f32 = jnp.float32
    nrm = lambda k, shape, scale: jax.random.normal(k, shape, f32) * scale
    x = jax.random.normal(ks[0], (BATCH, SEQ, D_MODEL), f32)
    even_norm_g = 1.0 + nrm(ks[1], (N_EVEN, D_MODEL), 0.02)
    even_w_in = nrm(ks[2], (N_EVEN, D_MODEL, IN_EVEN), D_MODEL ** -0.5)
    gmlp_ln_g = 1.0 + nrm(ks[3], (N_EVEN, A_WIDTH), 0.02)
    gmlp_ln_b = nrm(ks[4], (N_EVEN, A_WIDTH), 0.02)
    gmlp_ws = nrm(ks[5], (N_EVEN, A_GROUPS, A_CHUNK, A_CHUNK), A_CHUNK ** -0.5)
    gmlp_bs = 1.0 + nrm(ks[6], (N_EVEN, A_GROUPS, A_CHUNK), 0.1)
    ssd_conv_w = nrm(ks[7], (N_EVEN, B_CONV, B_XBC), B_CONV ** -0.5)
    ssd_conv_b = nrm(ks[8], (N_EVEN, B_XBC), 0.02)
    dt0 = jnp.exp(jax.random.uniform(ks[9], (N_EVEN, B_HEADS), f32,
                                     np.log(1e-3).astype(np.float32), np.log(1e-1).astype(np.float32)))
    ssd_dt_bias = dt0 + jnp.log(-jnp.expm1(-dt0))
    ssd_a_log = jnp.log(jax.random.uniform(ks[10], (N_EVEN, B_HEADS), f32, 1.0, 16.0))
    ssd_d = 1.0 + nrm(ks[11], (N_EVEN, B_HEADS), 0.1)
    ssd_norm_g = 1.0 + nrm(ks[12], (N_EVEN, B_WIDTH), 0.02)
    even_w_out = nrm(ks[13], (N_EVEN, MIX_WIDTH, D_MODEL), MIX_WIDTH ** -0.5)
    odd_norm_g = 1.0 + nrm(ks[14], (N_ODD, D_MODEL), 0.02)
    odd_w_in = nrm(ks[15], (N_ODD, D_MODEL, IN_ODD), D_MODEL ** -0.5)
    sconv_w = nrm(ks[16], (N_ODD, C_CONV, C_WIDTH), C_CONV ** -0.5)
    odd_w_out = nrm(ks[17], (N_ODD, MIX_WIDTH, D_MODEL), MIX_WIDTH ** -0.5)
    final_norm_g = 1.0 + nrm(ks[18], (D_MODEL,), 0.02)
    return {"x": x, "even_norm_g": even_norm_g, "even_w_in": even_w_in,
            "gmlp_ln_g": gmlp_ln_g, "gmlp_ln_b": gmlp_ln_b, "gmlp_ws": gmlp_ws,
            "gmlp_bs": gmlp_bs, "ssd_conv_w": ssd_conv_w, "ssd_conv_b": ssd_conv_b,
            "ssd_dt_bias": ssd_dt_bias, "ssd_a_log": ssd_a_log, "ssd_d": ssd_d,
            "ssd_norm_g": ssd_norm_g, "even_w_out": even_w_out,
            "odd_norm_g": odd_norm_g, "odd_w_in": odd_w_in, "sconv_w": sconv_w,
            "odd_w_out": odd_w_out, "final_norm_g": final_norm_g}


def reference(x, even_norm_g, even_w_in, gmlp_ln_g, gmlp_ln_b, gmlp_ws, gmlp_bs,
              ssd_conv_w, ssd_conv_b, ssd_dt_bias, ssd_a_log, ssd_d, ssd_norm_g,
              even_w_out, odd_norm_g, odd_w_in, sconv_w, odd_w_out, final_norm_g):
    for layer in range(DEPTH):
        i = layer // 2
        if layer % 2 == 0:
            x = even_layer(x, even_norm_g[i], even_w_in[i], gmlp_ln_g[i], gmlp_ln_b[i],
                           gmlp_ws[i], gmlp_bs[i], ssd_conv_w[i], ssd_conv_b[i],
                           ssd_dt_bias[i], ssd_a_log[i], ssd_d[i], ssd_norm_g[i],
                           even_w_out[i])
        else:
            x = odd_layer(x, odd_norm_g[i], odd_w_in[i], sconv_w[i], odd_w_out[i])
    return rms_norm(x, final_norm_g)
```

```python
import numpy as np
from contextlib import ExitStack
import concourse.bass as bass
import concourse.mybir as mybir
from concourse.bass_utils import run_bass_kernel_spmd

F32 = mybir.dt.float32
BF16 = mybir.dt.bfloat16
AF = mybir.ActivationFunctionType
ALU = mybir.AluOpType

COMPUTE = ("tensor", "vector", "scalar", "gpsimd")
DMAQ = ("sync", "gpsimd", "scalar")
NDMASEM = {"sync": 16, "gpsimd": 12, "scalar": 8}

_STOP = {}
T = 2048
NT = 16
D = 2048
EPS = 1e-5


class Prog:
    def __init__(self, nc, stack):
        self.nc = nc
        self.streams = {e: [] for e in ("sync", "tensor", "vector", "scalar", "gpsimd")}
        self.sem = {e: stack.enter_context(nc.semaphore("c_" + e)) for e in COMPUTE}
        self.dsem = {q: [stack.enter_context(nc.semaphore("d_%s%d" % (q, i))) for i in range(NDMASEM[q])]
                     for q in DMAQ}
        self.cnt = {e: 0 for e in COMPUTE}
        self.dcnt = {q: 0 for q in DMAQ}
        self.dsem_last = {q: [None] * NDMASEM[q] for q in DMAQ}
        self.known = {e: {} for e in self.streams}
        self.bufs = {}
        self.out_tokens = []

    def _need(self, stream, tok, waits):
        if tok is None:
            return
        key, val, snap = tok
        kn = self.known[stream]
        if kn.get(key, 0) >= val:
            return
        waits.append((key, val))
        kn[key] = val
        for k2, v2 in snap.items():
            if kn.get(k2, 0) < v2:
                kn[k2] = v2

    def _deps(self, stream, reads, writes):
        waits = []
        for k in reads:
            st = self.bufs.get(k)
            if st is not None:
                self._need(stream, st[0], waits)
        for k in writes:
            st = self.bufs.get(k)
            if st is not None:
                self._need(stream, st[0], waits)
                for t in st[1]:
                    self._need(stream, t, waits)
        best = {}
        for k, v in waits:
            if best.get(k, 0) < v:
                best[k] = v
        return list(best.items())

    def _record(self, tok, reads, writes):
        for k in reads:
            st = self.bufs.setdefault(k, [None, []])
            st[1].append(tok)
        for k in writes:
            self.bufs[k] = [tok, []]

    def op(self, eng, fn, reads=(), writes=()):
        waits = self._deps(eng, reads, writes)
        self.cnt[eng] += 1
        n = self.cnt[eng]
        key = ("c", eng)
        if eng == "tensor":
            self.known[eng][key] = n
        snap = dict(self.known[eng])
        snap[key] = n
        tok = (key, n, snap)
        self.streams[eng].append((fn, waits, key, 1))
        self._record(tok, reads, writes)
        return tok

    def dma(self, q, out, in_, reads=(), writes=(), is_output=False):
        waits = self._deps(q, reads, writes)
        i = self.dcnt[q]
        self.dcnt[q] += 1
        slot = i % NDMASEM[q]
        w2 = []
        self._need(q, self.dsem_last[q][slot], w2)
        waits = waits + w2
        key = ("d", q, slot)
        val = 16 * (i // NDMASEM[q] + 1)
        tok = (key, val, dict(self.known[q]))
        self.dsem_last[q][slot] = tok
        fn = (lambda e, o=out, s=in_: e.dma_start(out=o, in_=s))
        self.streams[q].append((fn, waits, key, 16))
        self._record(tok, reads, writes)
        if is_output:
            self.out_tokens.append(tok)
        return tok

    def _semh(self, key):
        if key[0] == "c":
            return self.sem[key[1]]
        return self.dsem[key[1]][key[2]]

    def fence(self):
        toks = [(("c", e), self.cnt[e], {}) for e in COMPUTE if self.cnt[e]]
        toks += [t for q in DMAQ for t in self.dsem_last[q] if t is not None]
        for s in self.streams:
            waits = []
            for t in toks:
                self._need(s, t, waits)
            best = {}
            for k, v in waits:
                if best.get(k, 0) < v:
                    best[k] = v
            self.streams[s].append((None, list(best.items()), None, 0))
        self.bufs = {}

    def emit(self):
        nc = self.nc
        fin = []
        for t in self.out_tokens:
            self._need("sync", t, fin)
        for q in DMAQ:
            for t in self.dsem_last[q]:
                self._need("sync", t, fin)
        for e in COMPUTE:
            if self.cnt[e]:
                self._need("sync", (("c", e), self.cnt[e], {}), fin)
        best = {}
        for k, v in fin:
            if best.get(k, 0) < v:
                best[k] = v
        self.streams["sync"].append((None, list(best.items()), None, 0))
        with nc.Block() as block:
            for ename in ("sync", "tensor", "vector", "scalar", "gpsimd"):
                ops = self.streams[ename]
                if not any(fn is not None for fn, _, _, _ in ops) and ename != "sync":
                    continue

                def body(e, ops=ops):
                    for fn, waits, key, inc in ops:
                        for k, v in waits:
                            e.wait_ge(self._semh(k), v)
                        if fn is not None:
                            fn(e).then_inc(self._semh(key), inc)

                getattr(block, ename)(body)


class Ctx:
    def __init__(self):
        self.nc = bass.Bass("TRN2", target_bir_lowering=False)
        self.stack = ExitStack()
        self.p = Prog(self.nc, self.stack)
        self.scopes = [self.stack]

    def push(self):
        self.scopes.append(ExitStack())

    def pop(self):
        self.p.fence()
        self.scopes.pop().close()

    def sb(self, name, shape, dt):
        self.nalloc = getattr(self, "nalloc", 0) + 1
        return self.scopes[-1].enter_context(self.nc.sbuf_tensor("%s_%d" % (name, self.nalloc), shape, dt))

    def ps(self, name):
        return self.stack.enter_context(self.nc.psum_tensor(name, [128, 512], F32))

    def din(self, name, shape, dt=F32):
        return self.nc.dram_tensor(name, shape, dt, kind="ExternalInput").ap()

    def dout(self, name, shape, dt=F32):
        return self.nc.dram_tensor(name, shape, dt, kind="ExternalOutput").ap()

    def dscr(self, name, shape, dt):
        return self.nc.dram_tensor(name, shape, dt, kind="Internal").ap()

    def bc(self, name, ap_row, n, q="sync"):
        t = self.sb(name, [128, n], F32)
        self.p.dma(q, t[:], ap_row.partition_broadcast(128), writes=[name])
        return t

    def load(self, name, ap, shape, dt=F32, q="sync"):
        t = self.sb(name, shape, dt)
        self.p.dma(q, t[:], ap, writes=[name])
        return t


def norm_transpose(c, src, ntiles, gB, xnT, col0, idb, PS, tag, gkey):
    p = c.p
    for i in range(ntiles):
        s = i % 2
        xt, xs, ss, ms, rstd, junk = (c.xt[s], c.xs[s], c.ss[s], c.ms[s], c.rstd[s], c.junk)
        p.dma("sync", xt[:], src[i * 128:(i + 1) * 128, :], writes=["xt%d" % s])
        p.op("scalar", lambda e, xt=xt: e.activation(out=junk[:], in_=xt[:], func=AF.Square),
             reads=["xt%d" % s], writes=["junk"])
        p.op("vector", lambda e, ss=ss: e.tensor_reduce(out=ss[:], in_=junk[:], axis=mybir.AxisListType.X, op=ALU.add),
             reads=["junk"], writes=["ss%d" % s])
        p.op("vector", lambda e, ss=ss, ms=ms: e.tensor_scalar(out=ms[:], in0=ss[:], scalar1=1.0 / D, scalar2=EPS,
                                                              op0=ALU.mult, op1=ALU.add),
             reads=["ss%d" % s], writes=["ms%d" % s])
        p.op("scalar", lambda e, ms=ms: e.activation(out=ms[:], in_=ms[:], func=AF.Sqrt),
             reads=["ms%d" % s], writes=["ms%d" % s])
        p.op("vector", lambda e, ms=ms, rstd=rstd: e.reciprocal(out=rstd[:], in_=ms[:]),
             reads=["ms%d" % s], writes=["rstd%d" % s])
        p.op("vector", lambda e, xt=xt, xs=xs, rstd=rstd: e.scalar_tensor_tensor(
            out=xs[:], in0=xt[:], scalar=rstd[:], in1=gB[:], op0=ALU.mult, op1=ALU.mult),
            reads=["xt%d" % s, "rstd%d" % s, gkey], writes=["xs%d" % s])
        for h in range(2):
            pt = PS[h][:].bitcast(BF16)
            for j in range(8):
                kc = h * 8 + j
                p.op("tensor", lambda e, pt=pt, j=j, kc=kc, xs=xs: e.transpose(
                    out=pt[:, j * 128:(j + 1) * 128], in_=xs[:, kc * 128:(kc + 1) * 128], identity=idb[:]),
                    reads=["xs%d" % s, "idb"], writes=["PS%d" % h])
            dst = xnT[:, h * 8:(h + 1) * 8, col0 + i * 128: col0 + (i + 1) * 128]
            srcp = pt.rearrange("p (a b) -> p a b", b=128)
            if h == 0:
                p.op("scalar", lambda e, dst=dst, srcp=srcp: e.copy(out=dst, in_=srcp),
                     reads=["PS%d" % h], writes=[(tag, i)])
            else:
                p.op("vector", lambda e, dst=dst, srcp=srcp: e.tensor_copy(out=dst, in_=srcp),
                     reads=["PS%d" % h], writes=[(tag, i)])


def alloc_norm(c):
    c.xt = [c.sb("xt%d" % s, [128, D], F32) for s in range(2)]
    c.xs = [c.sb("xs%d" % s, [128, D], BF16) for s in range(2)]
    c.ss = [c.sb("ss%d" % s, [128, 1], F32) for s in range(2)]
    c.ms = [c.sb("ms%d" % s, [128, 1], F32) for s in range(2)]
    c.rstd = [c.sb("rstd%d" % s, [128, 1], F32) for s in range(2)]
    c.junk = c.sb("junk", [128, D], F32)


def load_w(c, wb, slot, w, col0, ncols, dst0):
    c.p.dma("gpsimd", wb[:, :, dst0:dst0 + ncols],
            w[:, col0:col0 + ncols].rearrange("(kc p) f -> p kc f", p=128),
            writes=[("w", slot)])


def build_l0(mode, stop=None):
    c = Ctx()
    nc, p = c.nc, c.p
    full = mode == "l0"
    x = c.din("x", [T, D])
    xh = c.din("xh", [128, D])
    w_in = c.din("w_in", [D, 12320])
    norm_g = c.din("norm_g", [1, D])
    ident = c.din("ident", [128, 128])
    tri_le = c.din("tri_le", [128, 128])
    cw = c.din("cw", [128, 32 * 4])
    cb = c.din("cb", [128, 32])
    dt_bias = c.din("dt_bias", [1, 32])
    a_log = c.din("a_log", [1, 32])
    if full:
        w_out = c.din("w_out", [4096, D])
        s_all = c.din("s_all", [8, 128, 2048])
        d_all = c.din("d_all", [1, 8 * 32])
        cmask = c.din("cmask", [1, 8])
        d_skip = c.din("d_skip", [1, 2048])
        ssd_ng = c.din("ssd_ng", [1, 2048])
        ln_g = c.din("ln_g", [1, 2048])
        ln_b = c.din("ln_b", [1, 2048])
        wsT = c.din("wsT", [128, 8 * 128])
        bs = c.din("bs", [1, 8 * 128])
        x1 = c.dout("x1", [T, D])
        yT = c.dscr("yT", [4096, T], BF16)
        vscr = c.dscr("vscr", [T, 2048], BF16)
    else:
        s_out = c.dout("s_out", [128, 2048])
        d_out = c.dout("d_out", [1, 32])

    PS = [c.ps("PS%d" % i) for i in range(8)]
    idf = c.load("idf", ident, [128, 128])
    idb = c.sb("idb", [128, 128], BF16)
    p.op("vector", lambda e: e.tensor_copy(out=idb[:], in_=idf[:]), reads=["idf"], writes=["idb"])
    tri = c.load("tri", tri_le, [128, 128])
    onesf = c.sb("onesf", [128, 128], F32)
    p.op("vector", lambda e: e.memset(onesf[:], 1.0), writes=["onesf"])
    c.push()
    xnT = c.sb("xnT", [128, 16, 128 + T], BF16)
    wb = [c.sb("wb%d" % s, [128, 16, 512], BF16) for s in range(2)]
    c.push()
    alloc_norm(c)
    gB = c.bc("l0gB", norm_g, D)
    norm_transpose(c, xh, 1, gB, xnT, 0, idb, PS, "l0", "l0gB")
    norm_transpose(c, x, NT, gB, xnT, 128, idb, PS, "l0o", "l0gB")
    c.pop()
    XH = []
    XO = [[] for i in range(NT)]

    c.push()
    wdt = c.sb("wdt", [128, 16, 32], BF16)
    p.dma("gpsimd", wdt[:], w_in[:, 12288:12320].rearrange("(kc p) f -> p kc f", p=128), writes=["wdt"])
    dtbB = c.bc("dtbB", dt_bias, 32)
    aB = c.bc("aB", a_log, 32)
    p.op("scalar", lambda e: e.activation(out=aB[:], in_=aB[:], func=AF.Exp), reads=["aB"], writes=["aB"])
    p.op("vector", lambda e: e.tensor_scalar(out=aB[:], in0=aB[:], scalar1=-1.0, scalar2=None, op0=ALU.mult),
         reads=["aB"], writes=["aB"])
    dt_all = c.sb("dt_all", [128, NT, 32], F32)
    adt_all = c.sb("adt_all", [128, NT, 32], F32)
    acs_all = c.sb("acs_all", [128, NT, 32], F32)
    eacs_all = c.sb("eacs_all", [128, NT, 32], F32)
    cd_all = c.sb("cd_all", [128, NT, 32], F32)
    dtd_all = c.sb("dtd_all", [128, NT, 32], F32)
    tmp32 = c.sb("tmp32", [128, 32], F32)
    for i in range(NT):
        pd = PS[2 + (i % 2)]
        kpd = "PS%d" % (2 + (i % 2))
        for kc in range(16):
            p.op("tensor", lambda e, pd=pd, kc=kc, i=i: e.matmul(
                pd[:, 0:32], lhsT=xnT[:, kc, 128 + i * 128:128 + (i + 1) * 128], rhs=wdt[:, kc, :],
                start=(kc == 0), stop=(kc == 15)), reads=["wdt"], writes=[kpd])
        p.op("vector", lambda e, pd=pd, i=i: e.tensor_tensor(out=dt_all[:, i, :], in0=pd[:, 0:32], in1=dtbB[:], op=ALU.add),
             reads=[kpd, "dtbB"], writes=[("dt", i)])
        p.op("scalar", lambda e, i=i: e.activation(out=dt_all[:, i, :], in_=dt_all[:, i, :], func=AF.Exp),
             reads=[("dt", i)], writes=[("dt", i)])
        p.op("vector", lambda e, i=i: e.tensor_scalar(out=dt_all[:, i, :], in0=dt_all[:, i, :], scalar1=1.0, scalar2=None, op0=ALU.add),
             reads=[("dt", i)], writes=[("dt", i)])
        p.op("scalar", lambda e, i=i: e.activation(out=dt_all[:, i, :], in_=dt_all[:, i, :], func=AF.Ln),
             reads=[("dt", i)], writes=[("dt", i)])
        p.op("vector", lambda e, i=i: e.tensor_tensor(out=adt_all[:, i, :], in0=dt_all[:, i, :], in1=aB[:], op=ALU.mult),
             reads=[("dt", i), "aB"], writes=[("adt", i)])
        p.op("tensor", lambda e, pd=pd, i=i: e.matmul(pd[:, 64:96], lhsT=tri[:], rhs=adt_all[:, i, :], start=True, stop=True),
             reads=[("adt", i), "tri"], writes=[kpd])
        p.op("tensor", lambda e, pd=pd, i=i: e.matmul(pd[:, 128:160], lhsT=onesf[:], rhs=adt_all[:, i, :], start=True, stop=True),
             reads=[("adt", i), "onesf"], writes=[kpd])
        p.op("scalar", lambda e, pd=pd, i=i: e.copy(out=acs_all[:, i, :], in_=pd[:, 64:96]), reads=[kpd], writes=[("acs", i)])
        p.op("scalar", lambda e, pd=pd, i=i: e.activation(out=eacs_all[:, i, :], in_=pd[:, 64:96], func=AF.Exp),
             reads=[kpd], writes=[("eacs", i)])
        p.op("scalar", lambda e, pd=pd, i=i: e.activation(out=cd_all[:, i, :], in_=pd[:, 128:160], func=AF.Exp),
             reads=[kpd], writes=[("cd", i)])
        p.op("vector", lambda e, pd=pd, i=i: e.tensor_tensor(out=tmp32[:], in0=pd[:, 128:160], in1=acs_all[:, i, :], op=ALU.subtract),
             reads=[kpd, ("acs", i)], writes=["tmp32"])
        p.op("scalar", lambda e: e.activation(out=tmp32[:], in_=tmp32[:], func=AF.Exp), reads=["tmp32"], writes=["tmp32"])
        p.op("vector", lambda e, i=i: e.tensor_tensor(out=dtd_all[:, i, :], in0=tmp32[:], in1=dt_all[:, i, :], op=ALU.mult),
             reads=["tmp32", ("dt", i)], writes=[("dtd", i)])
    if not full:
        dsum = c.sb("dsum", [128, 32], F32)
        p.op("vector", lambda e: e.tensor_copy(out=dsum[:], in_=cd_all[:, 0, :]), reads=[("cd", 0)], writes=["dsum"])
        for i in range(1, NT):
            p.op("vector", lambda e, i=i: e.tensor_tensor(out=dsum[:], in0=dsum[:], in1=cd_all[:, i, :], op=ALU.mult),
                 reads=["dsum", ("cd", i)], writes=["dsum"])
        p.dma("sync", d_out, dsum[0:1, :], reads=["dsum"], is_output=True)

    cwt = c.load("cwt", cw, [128, 128])
    cbt = c.load("cbt", cb, [128, 32])
    xpre = c.sb("xpre", [128, 3 + T], F32)
    cacc = c.sb("cacc", [128, T], F32)
    xcT = c.sb("xcT", [128, T], BF16)
    BT = c.sb("BT", [128, T], BF16)
    CT = c.sb("CT", [128, T], BF16)
    x_tok = c.sb("x_tok", [128, NT, 256], BF16)
    B_tok = c.sb("B_tok", [128, NT, 128], BF16)
    S = c.sb("S", [128, 2048], F32)
    Sbf = c.sb("Sbf", [128, 256], BF16)
    Stmp = c.sb("Stmp", [128, 256], F32)
    xdt = [c.sb("xdt%d" % s, [128, 256], BF16) for s in range(2)]
    xdtd = [c.sb("xdtd%d" % s, [128, 256], BF16) for s in range(2)]
    if full:
        gz = c.sb("gz", [128, NT, 256], BF16)
        cbTm = c.sb("cbTm", [128, 128], F32)
        Rm = [c.sb("Rm%d" % s, [128, 128], F32) for s in range(2)]
        Dm = [c.sb("Dm%d" % s, [128, 128], F32) for s in range(2)]
        MT = [c.sb("MT%d" % s, [128, 128], BF16) for s in range(2)]
        y1 = c.sb("y1", [128, 256], F32)
        y2 = c.sb("y2", [128, 256], F32)
        yg = c.sb("yg", [128, 256], F32)
        yb = c.sb("yb", [128, 256], BF16)
        ybT = [c.sb("ybT%d" % s, [128, 2, 128], BF16) for s in range(2)]
        dskB = c.sb("dskB", [128, 256], F32)
        ngB = c.sb("ngB", [128, 256], F32)
        gss = c.sb("gss", [128, 1], F32)
        grs = c.sb("grs", [128, 1], F32)
        dallB = c.bc("dallB", d_all, 8 * 32)
        cmB = c.bc("cmB", cmask, 8)
        fB = c.sb("fB", [128, 8 * 32], F32)
        for r in range(8):
            p.op("vector", lambda e, r=r: e.tensor_scalar(out=fB[:, r * 32:(r + 1) * 32], in0=dallB[:, r * 32:(r + 1) * 32],
                                                          scalar1=-1.0, scalar2=cmB[:, r:r + 1], op0=ALU.add, op1=ALU.mult),
                 reads=["dallB", "cmB"], writes=["fB"])
        p.op("vector", lambda e: e.tensor_scalar(out=fB[:], in0=fB[:], scalar1=1.0, scalar2=None, op0=ALU.add),
             reads=["fB"], writes=["fB"])
        p.op("vector", lambda e: e.memset(S[:], 0.0), writes=["S"])
        sld = cacc
        for r in range(8):
            p.dma("sync", sld[:], s_all[r], writes=["cacc"])
            p.op("vector", lambda e, r=r: e.tensor_tensor(
                out=S[:].rearrange("p (h q) -> p h q", q=64), in0=S[:].rearrange("p (h q) -> p h q", q=64),
                in1=fB[:, r * 32:(r + 1) * 32].unsqueeze(2).to_broadcast([128, 32, 64]), op=ALU.mult),
                reads=["S", "fB"], writes=["S"])
            p.op("vector", lambda e, r=r: e.scalar_tensor_tensor(out=S[:], in0=sld[:], scalar=cmB[:, r:r + 1], in1=S[:],
                                                                 op0=ALU.mult, op1=ALU.add),
                 reads=["S", "cacc", "cmB"], writes=["S"])
    else:
        p.op("vector", lambda e: e.memset(S[:], 0.0), writes=["S"])

    XB = 8192
    for g in range(8):
        w = wb[0]
        wz = wb[1]
        load_w(c, w, 0, w_in, XB + g * 256, 256, 0)
        load_w(c, w, 0, w_in, XB + 2048 + g * 128, 128, 256)
        if full:
            load_w(c, w, 0, w_in, XB + 3072 + g * 128, 128, 384)
            load_w(c, wz, 1, w_in, 6144 + g * 256, 256, 0)
            p.dma("sync", dskB[:], d_skip[:, g * 256:(g + 1) * 256].partition_broadcast(128), writes=["dskB"])
            p.dma("sync", ngB[:], ssd_ng[:, g * 256:(g + 1) * 256].partition_broadcast(128), writes=["ngB"])
        wk = ("w", 0)
        chunks = [(0, "x0"), (128, "x1"), (256, "B")] + ([(384, "C")] if full else [])
        for (wc0, kind) in chunks:
            cidx = {"x0": 2 * g, "x1": 2 * g + 1, "B": 16 + g, "C": 24 + g}[kind]
            for tg in range(4):
                pb = PS[tg % 2]
                kb = "PS%d" % (tg % 2)
                for kc in range(16):
                    p.op("tensor", lambda e, pb=pb, kc=kc, tg=tg, wc0=wc0, w=w: e.matmul(
                        pb[:], lhsT=w[:, kc, wc0:wc0 + 128], rhs=xnT[:, kc, 128 + tg * 512:128 + (tg + 1) * 512],
                        start=(kc == 0), stop=(kc == 15)), reads=[wk], writes=[kb])
                p.op("scalar", lambda e, pb=pb, tg=tg: e.copy(out=xpre[:, 3 + tg * 512:3 + (tg + 1) * 512], in_=pb[:]),
                     reads=[kb], writes=["xpre"])
            pb = PS[0]
            for kc in range(16):
                p.op("tensor", lambda e, pb=pb, kc=kc, wc0=wc0, w=w: e.matmul(
                    pb[:, 0:3], lhsT=w[:, kc, wc0:wc0 + 128], rhs=xnT[:, kc, 125:128],
                    start=(kc == 0), stop=(kc == 15)), reads=[wk], writes=["PS0"])
            p.op("scalar", lambda e, pb=pb: e.copy(out=xpre[:, 0:3], in_=pb[:, 0:3]), reads=["PS0"], writes=["xpre"])
            p.op("vector", lambda e, cidx=cidx: e.tensor_scalar(
                out=cacc[:], in0=xpre[:, 0:T], scalar1=cwt[:, cidx * 4:cidx * 4 + 1], scalar2=cbt[:, cidx:cidx + 1],
                op0=ALU.mult, op1=ALU.add), reads=["xpre", "cwt", "cbt"], writes=["cacc"])
            for k in range(1, 4):
                p.op("vector", lambda e, cidx=cidx, k=k: e.scalar_tensor_tensor(
                    out=cacc[:], in0=xpre[:, k:k + T], scalar=cwt[:, cidx * 4 + k:cidx * 4 + k + 1], in1=cacc[:],
                    op0=ALU.mult, op1=ALU.add), reads=["xpre", "cwt", "cacc"], writes=["cacc"])
            dstT = {"x0": xcT, "x1": xcT, "B": BT, "C": CT}[kind]
            dkey = {"x0": "xcT", "x1": "xcT", "B": "BT", "C": "CT"}[kind]
            p.op("scalar", lambda e, dstT=dstT: e.activation(out=dstT[:], in_=cacc[:], func=AF.Silu),
                 reads=["cacc"], writes=[dkey])
            if stop == "dbg1":
                dbg1 = c.dout("dbg1", [128, 3 + T])
                dbg2 = c.dout("dbg2", [128, T])
                dbg3 = c.dout("dbg3", [128, 16 * 128], BF16)
                p.dma("sync", dbg1, xpre[:], reads=["xpre"], is_output=True)
                p.dma("sync", dbg2, cacc[:], reads=["cacc"], is_output=True)
                p.dma("sync", dbg3.rearrange("p (a b) -> p a b", b=128), xnT[:, :, 0:128], is_output=True)
                c.pop(); c.pop(); p.emit()
                return c
            if kind != "C":
                for h in range(2):
                    pt = PS[2 + h][:].bitcast(BF16)
                    kpt = "PS%d" % (2 + h)
                    for j in range(8):
                        ti = h * 8 + j
                        p.op("tensor", lambda e, pt=pt, j=j, ti=ti, dstT=dstT: e.transpose(
                            out=pt[:, j * 128:(j + 1) * 128], in_=dstT[:, ti * 128:(ti + 1) * 128], identity=idb[:]),
                            reads=[dkey, "idb"], writes=[kpt])
                    srcp = pt.rearrange("p (a b) -> p a b", b=128)
                    if kind == "B":
                        dst = B_tok[:, h * 8:(h + 1) * 8, :]
                        dk = "B_tok"
                    else:
                        o = 0 if kind == "x0" else 128
                        dst = x_tok[:, h * 8:(h + 1) * 8, o:o + 128]
                        dk = "x_tok"
                    p.op("vector", lambda e, dst=dst, srcp=srcp: e.tensor_copy(out=dst, in_=srcp),
                         reads=[kpt], writes=[dk])
        if full:
            for i in range(NT):
                pb = PS[i % 2]
                kb = "PS%d" % (i % 2)
                for kc in range(16):
                    p.op("tensor", lambda e, pb=pb, kc=kc, i=i, wz=wz: e.matmul(
                        pb[:, 0:256], lhsT=xnT[:, kc, 128 + i * 128:128 + (i + 1) * 128], rhs=wz[:, kc, 0:256],
                        start=(kc == 0), stop=(kc == 15)), reads=[("w", 1)], writes=[kb])
                p.op("scalar", lambda e, pb=pb, i=i: e.activation(out=gz[:, i, :], in_=pb[:, 0:256], func=AF.Silu),
                     reads=[kb], writes=[("gz", i)])
        Sg = S[:, g * 256:(g + 1) * 256]
        skey = ("S", g)
        p.op("scalar", lambda e, Sg=Sg: e.copy(out=Sbf[:], in_=Sg), reads=["S"], writes=["Sbf", skey])
        for ci in range(NT):
            s2 = ci % 2
            h4 = slice(4 * g, 4 * g + 4)
            xv = x_tok[:, ci, :].rearrange("p (h q) -> p h q", q=64)
            p.op("vector", lambda e, s2=s2, ci=ci, xv=xv, h4=h4: e.tensor_tensor(
                out=xdt[s2][:].rearrange("p (h q) -> p h q", q=64), in0=xv,
                in1=dt_all[:, ci, h4].unsqueeze(2).to_broadcast([128, 4, 64]), op=ALU.mult),
                reads=["x_tok", ("dt", ci)], writes=["xdt%d" % s2])
            p.op("gpsimd", lambda e, s2=s2, ci=ci, xv=xv, h4=h4: e.tensor_tensor(
                out=xdtd[s2][:].rearrange("p (h q) -> p h q", q=64), in0=xv,
                in1=dtd_all[:, ci, h4].unsqueeze(2).to_broadcast([128, 4, 64]), op=ALU.mult),
                reads=["x_tok", ("dtd", ci)], writes=["xdtd%d" % s2])
            cs = slice(ci * 128, (ci + 1) * 128)
            if full:
                p.op("tensor", lambda e, cs=cs: e.matmul(PS[4][:, 0:128], lhsT=BT[:, cs], rhs=CT[:, cs], start=True, stop=True),
                     reads=["BT", "CT"], writes=["PS4"])
                p.op("vector", lambda e: e.tensor_tensor(out=cbTm[:], in0=PS[4][:, 0:128], in1=tri[:], op=ALU.mult),
                     reads=["PS4", "tri"], writes=["cbTm"])
                for hh in range(4):
                    h = 4 * g + hh
                    s3 = hh % 2
                    p.op("gpsimd", lambda e, h=h, ci=ci, s3=s3: e.tensor_scalar(
                        out=Rm[s3][:], in0=idf[:], scalar1=acs_all[:, ci, h:h + 1], scalar2=None, op0=ALU.mult),
                        reads=["idf", ("acs", ci)], writes=["Rm%d" % s3])
                    p.op("tensor", lambda e, s3=s3: e.matmul(
                        PS[5][:, s3 * 128:(s3 + 1) * 128], lhsT=onesf[:], rhs=Rm[s3][:],
                        start=True, stop=True), reads=["onesf", "Rm%d" % s3], writes=[("PS5", s3)])
                    p.op("vector", lambda e, h=h, ci=ci, s3=s3: e.tensor_scalar(
                        out=Dm[s3][:], in0=PS[5][:, s3 * 128:(s3 + 1) * 128], scalar1=acs_all[:, ci, h:h + 1], scalar2=0.0,
                        op0=ALU.subtract, op1=ALU.min), reads=[("PS5", s3), ("acs", ci)], writes=["Dm%d" % s3])
                    p.op("scalar", lambda e, s3=s3: e.activation(out=Dm[s3][:], in_=Dm[s3][:], func=AF.Exp),
                         reads=["Dm%d" % s3], writes=["Dm%d" % s3])
                    p.op("gpsimd", lambda e, s3=s3: e.tensor_tensor(out=MT[s3][:], in0=cbTm[:], in1=Dm[s3][:], op=ALU.mult),
                         reads=["cbTm", "Dm%d" % s3], writes=["MT%d" % s3])
                    p.op("tensor", lambda e, s3=s3, s2=s2, hh=hh: e.matmul(
                        PS[6][:, hh * 64:(hh + 1) * 64], lhsT=MT[s3][:], rhs=xdt[s2][:, hh * 64:(hh + 1) * 64],
                        start=True, stop=True), reads=["MT%d" % s3, "xdt%d" % s2], writes=["PS6"])
                p.op("tensor", lambda e, cs=cs: e.matmul(PS[7][:, 0:256], lhsT=CT[:, cs], rhs=Sbf[:], start=True, stop=True),
                     reads=["CT", "Sbf"], writes=[("PS7", 0)])
            p.op("tensor", lambda e, ci=ci, s2=s2: e.matmul(PS[7][:, 256:512], lhsT=B_tok[:, ci, :], rhs=xdtd[s2][:],
                                                           start=True, stop=True),
                 reads=["B_tok", "xdtd%d" % s2], writes=[("PS7", 1)])
            if full:
                p.op("vector", lambda e, ci=ci, h4=h4: e.tensor_tensor(
                    out=y1[:].rearrange("p (h q) -> p h q", q=64), in0=PS[7][:, 0:256].rearrange("p (h q) -> p h q", q=64),
                    in1=eacs_all[:, ci, h4].unsqueeze(2).to_broadcast([128, 4, 64]), op=ALU.mult),
                    reads=[("PS7", 0), ("eacs", ci)], writes=["y1"])
                p.op("gpsimd", lambda e, ci=ci: e.tensor_tensor(out=y2[:], in0=x_tok[:, ci, :], in1=dskB[:], op=ALU.mult),
                     reads=["x_tok", "dskB"], writes=["y2"])
                p.op("gpsimd", lambda e: e.tensor_tensor(out=y2[:], in0=y2[:], in1=y1[:], op=ALU.add),
                     reads=["y1", "y2"], writes=["y2"])
                p.op("vector", lambda e: e.tensor_tensor(out=yg[:], in0=PS[6][:, 0:256], in1=y2[:], op=ALU.add),
                     reads=["PS6", "y2"], writes=["yg"])
                p.op("gpsimd", lambda e, ci=ci: e.tensor_tensor(out=yg[:], in0=yg[:], in1=gz[:, ci, :], op=ALU.mult),
                     reads=["yg", ("gz", ci)], writes=["yg"])
                p.op("scalar", lambda e: e.activation(out=y1[:], in_=yg[:], func=AF.Square), reads=["yg"], writes=["y1"])
                p.op("vector", lambda e: e.tensor_reduce(out=gss[:], in_=y1[:], axis=mybir.AxisListType.X, op=ALU.add),
                     reads=["y1"], writes=["gss"])
                p.op("vector", lambda e: e.tensor_scalar(out=gss[:], in0=gss[:], scalar1=1.0 / 256, scalar2=EPS,
                                                         op0=ALU.mult, op1=ALU.add), reads=["gss"], writes=["gss"])
                p.op("scalar", lambda e: e.activation(out=gss[:], in_=gss[:], func=AF.Sqrt), reads=["gss"], writes=["gss"])
                p.op("vector", lambda e: e.reciprocal(out=grs[:], in_=gss[:]), reads=["gss"], writes=["grs"])
                p.op("vector", lambda e: e.scalar_tensor_tensor(out=yb[:], in0=yg[:], scalar=grs[:], in1=ngB[:],
                                                                op0=ALU.mult, op1=ALU.mult), reads=["yg", "grs", "ngB"], writes=["yb"])
                pt = PS[4][:].bitcast(BF16)
                for f in range(2):
                    p.op("tensor", lambda e, pt=pt, f=f: e.transpose(out=pt[:, 512 + f * 128:512 + (f + 1) * 128],
                                                                     in_=yb[:, f * 128:(f + 1) * 128], identity=idb[:]),
                         reads=["yb", "idb"], writes=["PS4"])
                p.op("scalar", lambda e, pt=pt, s2=s2: e.copy(out=ybT[s2][:], in_=pt[:, 512:768].rearrange("p (a b) -> p a b", b=128)),
                     reads=["PS4"], writes=["ybT%d" % s2])
                r0 = 2048 + g * 256
                p.dma("sync", yT[r0:r0 + 256, cs].rearrange("(f p) t -> p f t", p=128), ybT[s2][:], reads=["ybT%d" % s2], writes=[])
            p.op("gpsimd", lambda e, Sg=Sg, ci=ci, h4=h4: e.tensor_tensor(
                out=Stmp[:].rearrange("p (h q) -> p h q", q=64), in0=Sg.rearrange("p (h q) -> p h q", q=64),
                in1=cd_all[:, ci, h4].unsqueeze(2).to_broadcast([128, 4, 64]), op=ALU.mult),
                reads=[skey, ("cd", ci)], writes=["Stmp"])
            p.op("vector", lambda e, Sg=Sg: e.tensor_tensor(out=Sg, in0=PS[7][:, 256:512], in1=Stmp[:], op=ALU.add),
                 reads=[("PS7", 1), "Stmp"], writes=[skey])
            if full:
                p.op("scalar", lambda e, Sg=Sg: e.copy(out=Sbf[:], in_=Sg), reads=[skey], writes=["Sbf"])
    if not full:
        p.dma("sync", s_out, S[:], reads=[("S", g) for g in range(8)] + ["S"], is_output=True)
        c.pop()
        c.pop()
        p.emit()
        return c
    c.pop()
    if stop == "ssd":
        c.pop()
        p.emit()
        return c

    c.push()
    wsT_t = c.load("wsT_t", wsT, [128, 8 * 128])
    for g in range(8):
        p.op("vector", lambda e, g=g: e.tensor_tensor(out=wsT_t[:, g * 128:(g + 1) * 128], in0=wsT_t[:, g * 128:(g + 1) * 128],
                                                      in1=tri[:], op=ALU.mult), reads=["wsT_t", "tri"], writes=["wsT_t"])
    wsb = c.sb("wsb", [128, 8 * 128], BF16)
    p.op("vector", lambda e: e.tensor_copy(out=wsb[:], in_=wsT_t[:]), reads=["wsT_t"], writes=["wsb"])
    bsB = c.bc("bsB", bs, 8 * 128)
    vsum = c.sb("vsum", [128, NT, 4], F32)
    vsq = c.sb("vsq", [128, NT, 4], F32)
    vjunk = c.sb("vjunk", [128, 512], F32)
    mv = c.sb("mv", [128, NT, 2], F32)
    lrs = c.sb("lrs", [128, NT], F32)
    lnb2 = c.sb("lnb2", [128, NT], F32)
    vst = [c.sb("vst%d" % s, [128, 512], BF16) for s in range(2)]
    n = 0
    for blk in range(4):
        slot = blk % 2
        w = wb[slot]
        wk = ("w", slot)
        load_w(c, w, slot, w_in, 2048 + blk * 512, 512, 0)
        for i in range(NT):
            pb = PS[i % 2]
            kb = "PS%d" % (i % 2)
            for kc in range(16):
                p.op("tensor", lambda e, pb=pb, kc=kc, i=i, w=w: e.matmul(
                    pb[:], lhsT=xnT[:, kc, 128 + i * 128:128 + (i + 1) * 128], rhs=w[:, kc, 0:512],
                    start=(kc == 0), stop=(kc == 15)), reads=[wk], writes=[kb])
            s = n % 2
            n += 1
            p.op("scalar", lambda e, pb=pb, s=s: e.copy(out=vst[s][:], in_=pb[:]), reads=[kb], writes=["vst%d" % s])
            p.op("vector", lambda e, s=s, i=i, blk=blk: e.tensor_reduce(out=vsum[:, i, blk:blk + 1], in_=vst[s][:], axis=mybir.AxisListType.X, op=ALU.add),
                 reads=["vst%d" % s], writes=[("vsum", i, blk)])
            p.op("scalar", lambda e, s=s: e.activation(out=vjunk[:], in_=vst[s][:], func=AF.Square), reads=["vst%d" % s], writes=["vjunk"])
            p.op("vector", lambda e, i=i, blk=blk: e.tensor_reduce(out=vsq[:, i, blk:blk + 1], in_=vjunk[:], axis=mybir.AxisListType.X, op=ALU.add),
                 reads=["vjunk"], writes=[("vsq", i, blk)])
            p.dma("sync", vscr[i * 128:(i + 1) * 128, blk * 512:(blk + 1) * 512], vst[s][:], reads=["vst%d" % s], writes=["vscr"])
    if stop == "gmlp1":
        c.pop(); c.pop(); p.emit()
        return c
    for i in range(NT):
        rk = [("vsum", i, b) for b in range(4)] + [("vsq", i, b) for b in range(4)]
        p.op("vector", lambda e, i=i: e.tensor_reduce(out=mv[:, i, 0:1], in_=vsum[:, i, :], axis=mybir.AxisListType.X, op=ALU.add),
             reads=rk, writes=[("mv", i)])
        p.op("vector", lambda e, i=i: e.tensor_reduce(out=mv[:, i, 1:2], in_=vsq[:, i, :], axis=mybir.AxisListType.X, op=ALU.add),
             reads=rk, writes=[("mv", i)])
        p.op("vector", lambda e, i=i: e.tensor_scalar(out=mv[:, i, :], in0=mv[:, i, :], scalar1=1.0 / 2048, scalar2=None, op0=ALU.mult),
             reads=[("mv", i)], writes=[("mv", i)])
        p.op("vector", lambda e, i=i: e.tensor_tensor(out=lnb2[:, i:i + 1], in0=mv[:, i, 0:1], in1=mv[:, i, 0:1], op=ALU.mult),
             reads=[("mv", i)], writes=[("lnb2", i)])
        p.op("vector", lambda e, i=i: e.tensor_tensor(out=lrs[:, i:i + 1], in0=mv[:, i, 1:2], in1=lnb2[:, i:i + 1], op=ALU.subtract),
             reads=[("mv", i), ("lnb2", i)], writes=[("lrs", i)])
        p.op("vector", lambda e, i=i: e.tensor_scalar(out=lrs[:, i:i + 1], in0=lrs[:, i:i + 1], scalar1=EPS, scalar2=None, op0=ALU.add),
             reads=[("lrs", i)], writes=[("lrs", i)])
        p.op("scalar", lambda e, i=i: e.activation(out=lrs[:, i:i + 1], in_=lrs[:, i:i + 1], func=AF.Sqrt),
             reads=[("lrs", i)], writes=[("lrs", i)])
        p.op("vector", lambda e, i=i: e.reciprocal(out=lrs[:, i:i + 1], in_=lrs[:, i:i + 1]), reads=[("lrs", i)], writes=[("lrs", i)])
        p.op("vector", lambda e, i=i: e.scalar_tensor_tensor(out=lnb2[:, i:i + 1], in0=mv[:, i, 0:1], scalar=-1.0, in1=lrs[:, i:i + 1],
                                                             op0=ALU.mult, op1=ALU.mult), reads=[("mv", i), ("lrs", i)], writes=[("lnb2", i)])
    if stop == "gmlp2":
        c.pop(); c.pop(); p.emit()
        return c
    vg = c.sb("vg", [128, NT, 256], BF16)
    vnb = c.sb("vnb", [128, NT, 256], BF16)
    vtmp = c.sb("vtmp", [128, 256], F32)
    lngB = c.sb("lngB", [128, 256], F32)
    lnbB = c.sb("lnbB", [128, 256], F32)
    uT = c.sb("uT", [128, T], F32)
    gT = c.sb("gT", [128, T], F32)
    yaT = c.sb("yaT", [128, T], BF16)
    mixs = c.sb("mixs", [128, 512], F32)
    for g in range(8):
        slot = g % 2
        w = wb[slot]
        wk = ("w", slot)
        load_w(c, w, slot, w_in, g * 256, 256, 0)
        load_w(c, w, slot, w_in, 4096 + g * 256, 256, 256)
        p.dma("sync", vg[:], vscr[:, g * 256:(g + 1) * 256].rearrange("(i p) f -> p i f", p=128), reads=["vscr"], writes=["vg"])
        p.dma("sync", lngB[:], ln_g[:, g * 256:(g + 1) * 256].partition_broadcast(128), writes=["lngB"])
        p.dma("sync", lnbB[:], ln_b[:, g * 256:(g + 1) * 256].partition_broadcast(128), writes=["lnbB"])
        for i in range(NT):
            p.op("scalar", lambda e, i=i: e.activation(out=vtmp[:], in_=vg[:, i, :], func=AF.Identity, bias=lnb2[:, i:i + 1], scale=lrs[:, i:i + 1]),
                 reads=["vg", ("lnb2", i), ("lrs", i)], writes=["vtmp"])
            p.op("vector", lambda e: e.tensor_tensor(out=vtmp[:], in0=vtmp[:], in1=lngB[:], op=ALU.mult), reads=["vtmp", "lngB"], writes=["vtmp"])
            p.op("gpsimd", lambda e, i=i: e.tensor_tensor(out=vnb[:, i, :], in0=vtmp[:], in1=lnbB[:], op=ALU.add),
                 reads=["vtmp", "lnbB"], writes=["vnb"])
        for f in range(2):
            for which in range(2):
                wc0 = which * 256 + f * 128
                for tg in range(4):
                    pb = PS[tg % 2]
                    kb = "PS%d" % (tg % 2)
                    for kc in range(16):
                        p.op("tensor", lambda e, pb=pb, kc=kc, tg=tg, wc0=wc0, w=w: e.matmul(
                            pb[:], lhsT=w[:, kc, wc0:wc0 + 128], rhs=xnT[:, kc, 128 + tg * 512:128 + (tg + 1) * 512],
                            start=(kc == 0), stop=(kc == 15)), reads=[wk], writes=[kb])
                    ts_ = slice(tg * 512, (tg + 1) * 512)
                    if which == 0:
                        p.op("vector", lambda e, pb=pb, ts_=ts_: e.tensor_copy(out=uT[:, ts_], in_=pb[:]),
                             reads=[kb], writes=["uT"])
                    else:
                        p.op("scalar", lambda e, pb=pb, ts_=ts_: e.activation(out=gT[:, ts_], in_=pb[:], func=AF.Silu),
                             reads=[kb], writes=["gT"])
            p.op("gpsimd", lambda e: e.tensor_tensor(out=gT[:], in0=gT[:], in1=uT[:], op=ALU.mult),
                 reads=["gT", "uT"], writes=["gT"])
            for tq in range(4):
                pm = PS[2 + (tq % 2)]
                km = "PS%d" % (2 + (tq % 2))
                for j in range(4):
                    i = tq * 4 + j
                    p.op("tensor", lambda e, pm=pm, i=i, j=j, f=f, g=g: e.matmul(
                        pm[:, j * 128:(j + 1) * 128], lhsT=vnb[:, i, f * 128:(f + 1) * 128], rhs=wsb[:, g * 128:(g + 1) * 128],
                        start=True, stop=True), reads=["vnb", "wsb"], writes=[km])
                ts_ = slice(tq * 512, (tq + 1) * 512)
                p.op("vector", lambda e, pm=pm, g=g: e.tensor_tensor(
                    out=mixs[:].rearrange("p (a b) -> p a b", b=128), in0=pm[:].rearrange("p (a b) -> p a b", b=128),
                    in1=bsB[:, g * 128:(g + 1) * 128].unsqueeze(1).to_broadcast([128, 4, 128]), op=ALU.add),
                    reads=[km, "bsB"], writes=["mixs"])
                p.op("vector", lambda e, ts_=ts_: e.tensor_tensor(out=yaT[:, ts_], in0=mixs[:], in1=gT[:, ts_], op=ALU.mult),
                     reads=["mixs", "gT"], writes=["yaT"])
            r0 = g * 256 + f * 128
            p.dma("sync", yT[r0:r0 + 128, :], yaT[:], reads=["yaT"], writes=[])
    c.pop()
    c.pop()
    if stop == "gmlp":
        p.emit()
        return c
    c.push()
    out_proj(c, yT, w_out, x, x1, PS, True)
    c.pop()
    p.emit()
    return c


def out_proj(c, yT, w_out, xres, xout, PS, is_output):
    p = c.p
    ybuf = c.sb("ybuf", [128, 32, 1024], BF16)
    wo = [c.sb("wo%d" % s, [128, 32, 512], BF16) for s in range(2)]
    xr = [c.sb("xr%d" % s, [128, 512], F32) for s in range(2)]
    xo = [c.sb("xo%d" % s, [128, 512], F32) for s in range(2)]
    n = 0
    for half in range(2):
        t0 = half * 1024
        for f4 in range(8):
            p.dma("sync", ybuf[:, f4 * 4:(f4 + 1) * 4, :],
                  yT[f4 * 512:(f4 + 1) * 512, t0:t0 + 1024].rearrange("(f p) t -> p f t", p=128), writes=[("ybuf", f4)])
        for dblk in range(4):
            slot = (half * 4 + dblk) % 2
            for q4 in range(4):
                p.dma("gpsimd", wo[slot][:, q4 * 8:(q4 + 1) * 8, :],
                      w_out[q4 * 1024:(q4 + 1) * 1024, dblk * 512:(dblk + 1) * 512].rearrange("(fc p) d -> p fc d", p=128),
                      writes=[("wo", slot, q4)])
            for ti in range(8):
                s = n % 2
                n += 1
                tok0 = t0 + ti * 128
                p.dma("sync", xr[s][:], xres[tok0:tok0 + 128, dblk * 512:(dblk + 1) * 512], writes=["xr%d" % s])
                pb = PS[s]
                kb = "PS%d" % s
                for f in range(32):
                    p.op("tensor", lambda e, pb=pb, f=f, ti=ti, slot=slot: e.matmul(
                        pb[:], lhsT=ybuf[:, f, ti * 128:(ti + 1) * 128], rhs=wo[slot][:, f, :],
                        start=(f == 0), stop=(f == 31)),
                        reads=[("ybuf", f // 4), ("wo", slot, f // 8)], writes=[kb])
                p.op("vector", lambda e, pb=pb, s=s: e.tensor_tensor(out=xo[s][:], in0=pb[:], in1=xr[s][:], op=ALU.add),
                     reads=[kb, "xr%d" % s], writes=["xo%d" % s])
                p.dma("sync", xout[tok0:tok0 + 128, dblk * 512:(dblk + 1) * 512], xo[s][:], reads=["xo%d" % s],
                      writes=[], is_output=is_output)


_CACHE = {}


def _consts():
    ident = np.eye(128, dtype=np.float32)
    tri_le = np.triu(np.ones((128, 128), dtype=np.float32))
    return ident, tri_le


def run(nc, in_maps):
    import time
    t0 = time.time()
    r = run_bass_kernel_spmd(nc, in_maps, core_ids=list(range(8))).results
    print("launch wall %.1fs" % (time.time() - t0), flush=True)
    return r


def kernel(**inp):
    x = np.ascontiguousarray(inp["x"], dtype=np.float32)
    ident, tri_le = _consts()
    zeros_h = np.zeros((128, D), dtype=np.float32)
    xs, xhs = [], []
    for cidx in range(8):
        b, q = divmod(cidx, 4)
        xs.append(np.ascontiguousarray(x[b, q * T:(q + 1) * T]))
        xhs.append(zeros_h if q == 0 else np.ascontiguousarray(x[b, q * T - 128:q * T]))
    w_in0 = np.ascontiguousarray(inp["even_w_in"][0])
    cw = np.ascontiguousarray(inp["ssd_conv_w"][0].reshape(4, 32, 128).transpose(2, 1, 0).reshape(128, 128))
    cb = np.ascontiguousarray(inp["ssd_conv_b"][0].reshape(32, 128).T)
    base = {"w_in": w_in0, "norm_g": inp["even_norm_g"].reshape(1, D), "ident": ident, "tri_le": tri_le,
            "cw": cw, "cb": cb, "dt_bias": inp["ssd_dt_bias"].reshape(1, 32), "a_log": inp["ssd_a_log"].reshape(1, 32)}
    if "state" not in _CACHE:
        _CACHE["state"] = build_l0("state")
    r1 = run(_CACHE["state"].nc, [dict(base, x=xs[i], xh=xhs[i]) for i in range(8)])
    s_all = np.stack([r1[i]["s_out"] for i in range(8)])
    d_all = np.concatenate([r1[i]["d_out"].reshape(1, 32) for i in range(8)], axis=1)
    if "l0" not in _CACHE:
        import os
        _CACHE["l0"] = build_l0("l0", os.environ.get("L0STOP"))
    full = dict(base, w_out=np.ascontiguousarray(inp["even_w_out"][0]), s_all=s_all, d_all=d_all,
                d_skip=np.repeat(inp["ssd_d"][0], 64).reshape(1, 2048), ssd_ng=inp["ssd_norm_g"].reshape(1, 2048),
                ln_g=inp["gmlp_ln_g"].reshape(1, 2048), ln_b=inp["gmlp_ln_b"].reshape(1, 2048),
                wsT=np.ascontiguousarray(inp["gmlp_ws"][0].transpose(2, 0, 1).reshape(128, 1024)),
                bs=inp["gmlp_bs"].reshape(1, 1024))
    maps = []
    for i in range(8):
        b, q = divmod(i, 4)
        cm = np.zeros((1, 8), dtype=np.float32)
        cm[0, b * 4:b * 4 + q] = 1.0
        maps.append(dict(full, x=xs[i], xh=xhs[i], cmask=cm))
    r2 = run(_CACHE["l0"].nc, maps)
    x1s = [r2[i]["x1"] for i in range(8)]
    if _STOP.get("ret") == "x1":
        return np.stack([np.concatenate([x1s[b * 4 + q] for q in range(4)], axis=0) for b in range(2)])
    if "l1" not in _CACHE:
        _CACHE["l1"] = build_l1()
    tri_ge = np.ascontiguousarray(tri_le.T)
    scw = np.ascontiguousarray(inp["sconv_w"][0].reshape(3, 16, 128).transpose(2, 1, 0).reshape(128, 48))
    b1 = {"w_in": np.ascontiguousarray(inp["odd_w_in"][0]), "norm_g": inp["odd_norm_g"].reshape(1, D), "ident": ident,
          "tri_le": tri_le, "tri_ge": tri_ge, "scw": scw, "w_out": np.ascontiguousarray(inp["odd_w_out"][0]),
          "final_g": inp["final_norm_g"].reshape(1, D)}
    maps = []
    zeros_t = np.zeros((T, D), dtype=np.float32)
    for i in range(8):
        b, q = divmod(i, 4)
        maps.append(dict(b1, x1=x1s[i], x1h=(zeros_t if q == 0 else x1s[i - 1]),
                         hv=np.full((1, 1), 0.0 if q == 0 else 1.0, dtype=np.float32)))
    r3 = run(_CACHE["l1"].nc, maps)
    return np.stack([np.concatenate([r3[b * 4 + q]["out"] for q in range(4)], axis=0) for b in range(2)]).astype(np.float32)


def build_l1():
    c = Ctx()
    nc, p = c.nc, c.p
    x1 = c.din("x1", [T, D])
    x1h = c.din("x1h", [T, D])
    hv = c.din("hv", [1, 1])
    w_in = c.din("w_in", [D, 16384])
    norm_g = c.din("norm_g", [1, D])
    ident = c.din("ident", [128, 128])
    tri_le = c.din("tri_le", [128, 128])
    tri_ge = c.din("tri_ge", [128, 128])
    scw = c.din("scw", [128, 16 * 3])
    w_out = c.din("w_out", [4096, D])
    final_g = c.din("final_g", [1, D])
    out = c.dout("out", [T, D])
    kTh = c.dscr("kTh", [2048, T], BF16)
    vscr = c.dscr("vscr1", [2 * T, 2048], BF16)
    yT = c.dscr("yT1", [4096, T], BF16)
    x2 = c.dscr("x2", [T, D], F32)

    PS = [c.ps("PS%d" % i) for i in range(8)]
    idf = c.load("idf", ident, [128, 128])
    idb = c.sb("idb", [128, 128], BF16)
    p.op("vector", lambda e: e.tensor_copy(out=idb[:], in_=idf[:]), reads=["idf"], writes=["idb"])
    mk = c.sb("mk", [128, 256], F32)
    p.dma("sync", mk[:, 0:128], tri_ge, writes=["mk"])
    p.dma("sync", mk[:, 128:256], tri_le, writes=["mk2"])
    mask2 = c.sb("mask2", [128, 256], BF16)
    p.op("vector", lambda e: e.tensor_copy(out=mask2[:], in_=mk[:]), reads=["mk", "mk2"], writes=["mask2"])
    onesb = c.sb("onesb", [128, 128], BF16)
    p.op("vector", lambda e: e.memset(onesb[:], 1.0), writes=["onesb"])
    hvB = c.bc("hvB", hv, 1)
    xh2 = c.sb("xh2", [128, 16, 2], BF16)
    c.push()
    xnT = c.sb("xnT", [128, 16, T], BF16)
    wb = [c.sb("wb%d" % s, [128, 16, 512], BF16) for s in range(2)]

    c.push()
    alloc_norm(c)
    gB = c.bc("l1gB", norm_g, D)
    norm_transpose(c, x1h, NT, gB, xnT, 0, idb, PS, "h", "l1gB")
    c.pop()
    c.push()
    p.op("vector", lambda e: e.tensor_copy(out=xh2[:], in_=xnT[:, :, T - 2:T]), writes=["xh2"])
    kst = [c.sb("kst%d" % s, [128, 512], BF16) for s in range(2)]
    n = 0
    for hq in range(4):
        slot = hq % 2
        w = wb[slot]
        load_w(c, w, slot, w_in, 10240 + hq * 512, 512, 0)
        for hh in range(4):
            h = hq * 4 + hh
            for tg in range(4):
                s = n % 2
                n += 1
                pb = PS[s]
                for kc in range(16):
                    p.op("tensor", lambda e, pb=pb, kc=kc, tg=tg, hh=hh, w=w: e.matmul(
                        pb[:], lhsT=w[:, kc, hh * 128:(hh + 1) * 128], rhs=xnT[:, kc, tg * 512:(tg + 1) * 512],
                        start=(kc == 0), stop=(kc == 15)), reads=[("w", slot)], writes=["PS%d" % s])
                p.op("scalar", lambda e, pb=pb, s=s: e.copy(out=kst[s][:], in_=pb[:]), reads=["PS%d" % s], writes=["kst%d" % s])
                p.dma("sync", kTh[h * 128:(h + 1) * 128, tg * 512:(tg + 1) * 512], kst[s][:], reads=["kst%d" % s], writes=[])
    for hq in range(4):
        slot = hq % 2
        w = wb[slot]
        load_w(c, w, slot, w_in, 12288 + hq * 512, 512, 0)
        for i in range(NT):
            s = n % 2
            n += 1
            pb = PS[s]
            for kc in range(16):
                p.op("tensor", lambda e, pb=pb, kc=kc, i=i, w=w: e.matmul(
                    pb[:], lhsT=xnT[:, kc, i * 128:(i + 1) * 128], rhs=w[:, kc, 0:512],
                    start=(kc == 0), stop=(kc == 15)), reads=[("w", slot)], writes=["PS%d" % s])
            p.op("scalar", lambda e, pb=pb, s=s: e.copy(out=kst[s][:], in_=pb[:]), reads=["PS%d" % s], writes=["kst%d" % s])
            p.dma("sync", vscr[i * 128:(i + 1) * 128, hq * 512:(hq + 1) * 512], kst[s][:], reads=["kst%d" % s], writes=[])
    c.pop()

    c.push()
    alloc_norm(c)
    gB = c.bc("l1gB2", norm_g, D)
    norm_transpose(c, x1, NT, gB, xnT, 0, idb, PS, "o", "l1gB2")
    c.pop()

    c.push()
    scwt = c.load("scwt", scw, [128, 48])
    cT = c.sb("cT", [128, 2 + T], F32)
    chT = c.sb("chT", [128, 2 + T], F32)
    acc = c.sb("acc", [128, T], F32)
    bT = c.sb("bT", [128, T], F32)
    zs = c.sb("zs", [128, T], F32)
    ycT = c.sb("ycT", [128, T], BF16)
    n = 0
    for j in range(16):
        slot = j % 2
        w = wb[slot]
        wk = ("w", slot)
        for k4 in range(4):
            load_w(c, w, slot, w_in, k4 * 2048 + j * 128, 128, k4 * 128)
        for k4, kind in ((1, "c"), (2, "h"), (0, "b"), (3, "z")):
            wc0 = k4 * 128
            for tg in range(5):
                s = n % 2
                n += 1
                pb = PS[s]
                kb = "PS%d" % s
                if tg == 4 and kind in ("b", "z"):
                    continue
                for kc in range(16):
                    if tg < 4:
                        p.op("tensor", lambda e, pb=pb, kc=kc, tg=tg, wc0=wc0, w=w: e.matmul(
                            pb[:], lhsT=w[:, kc, wc0:wc0 + 128], rhs=xnT[:, kc, tg * 512:(tg + 1) * 512],
                            start=(kc == 0), stop=(kc == 15)), reads=[wk], writes=[kb])
                    else:
                        p.op("tensor", lambda e, pb=pb, kc=kc, wc0=wc0, w=w: e.matmul(
                            pb[:, 0:2], lhsT=w[:, kc, wc0:wc0 + 128], rhs=xh2[:, kc, :],
                            start=(kc == 0), stop=(kc == 15)), reads=[wk, "xh2"], writes=[kb])
                dsl = slice(2 + tg * 512, 2 + (tg + 1) * 512) if tg < 4 else slice(0, 2)
                psl = slice(0, 512) if tg < 4 else slice(0, 2)
                tsl = slice(tg * 512, (tg + 1) * 512)
                if kind == "c":
                    p.op("scalar", lambda e, pb=pb, dsl=dsl, psl=psl: e.copy(out=cT[:, dsl], in_=pb[:, psl]), reads=[kb], writes=["cT"])
                elif kind == "h":
                    p.op("vector", lambda e, pb=pb, dsl=dsl, psl=psl: e.tensor_tensor(out=chT[:, dsl], in0=pb[:, psl], in1=cT[:, dsl], op=ALU.mult),
                         reads=[kb, "cT"], writes=["chT"])
                elif kind == "b":
                    p.op("scalar", lambda e, pb=pb, tsl=tsl: e.copy(out=bT[:, tsl], in_=pb[:]), reads=[kb], writes=["bT"])
                else:
                    p.op("scalar", lambda e, pb=pb, tsl=tsl: e.activation(out=zs[:, tsl], in_=pb[:], func=AF.Silu), reads=[kb], writes=["zs"])
        p.op("vector", lambda e, j=j: e.tensor_scalar(out=acc[:], in0=chT[:, 0:T], scalar1=scwt[:, j * 3:j * 3 + 1], scalar2=None, op0=ALU.mult),
             reads=["chT", "scwt"], writes=["acc"])
        for k in (1, 2):
            p.op("vector", lambda e, j=j, k=k: e.scalar_tensor_tensor(out=acc[:], in0=chT[:, k:k + T], scalar=scwt[:, j * 3 + k:j * 3 + k + 1],
                                                                      in1=acc[:], op0=ALU.mult, op1=ALU.add),
                 reads=["chT", "scwt", "acc"], writes=["acc"])
        p.op("gpsimd", lambda e: e.tensor_tensor(out=zs[:], in0=zs[:], in1=bT[:], op=ALU.mult), reads=["zs", "bT"], writes=["zs"])
        p.op("gpsimd", lambda e: e.tensor_tensor(out=ycT[:], in0=zs[:], in1=acc[:], op=ALU.mult), reads=["zs", "acc"], writes=["ycT"])
        p.dma("sync", yT[j * 128:(j + 1) * 128, :], ycT[:], reads=["ycT"], writes=[])
    c.pop()
    if _STOP.get("l1") == "conv":
        c.pop(); p.emit()
        return c

    c.push()
    qT = [c.sb("qT%d" % s, [128, T], BF16) for s in range(2)]
    kT = [c.sb("kT%d" % s, [128, 2 * T], BF16) for s in range(2)]
    gzT = [c.sb("gzT%d" % s, [128, T], BF16) for s in range(2)]
    negc = [c.sb("negc%d" % s, [1, T], BF16) for s in range(2)]
    qn2 = c.sb("qn2", [1, T], F32)
    km8 = c.sb("km8", [1, 8], F32)
    krow = c.sb("krow", [1, 512], F32)
    kmax = c.sb("kmax", [1, 1], F32)
    sq = c.sb("sq", [128, 512], BF16)
    v_h = c.sb("v_h", [128, 69, 128], BF16)
    vst = [c.sb("vst%d" % s, [128, 4, 128], BF16) for s in range(2)]
    ONd = c.sb("ONd", [128, 2, T], F32)
    Eb = [c.sb("Eb%d" % s, [128, 256], BF16) for s in range(2)]
    Pb = [c.sb("Pb%d" % s, [128, 256], BF16) for s in range(2)]
    ydT = c.sb("ydT", [128, T], BF16)
    SCALE = float(128 ** -0.5)
    cnt = {"pa": 0, "v": 0}

    def stage_a(h):
        s = h % 2
        w = wb[s]
        wk = ("w", s)
        for k4 in range(4):
            load_w(c, w, s, w_in, 8192 + k4 * 2048 + h * 128, 128, k4 * 128)
        p.dma("sync", kT[s][:, 0:T], kTh[h * 128:(h + 1) * 128, :], writes=[("kTh", s)])
        for kind, wc0 in (("q", 0), ("k", 128), ("z", 384)):
            for tg in range(4):
                a = cnt["pa"] % 2
                cnt["pa"] += 1
                pb = PS[a]
                kb = "PS%d" % a
                for kc in range(16):
                    p.op("tensor", lambda e, pb=pb, kc=kc, tg=tg, wc0=wc0, w=w: e.matmul(
                        pb[:], lhsT=w[:, kc, wc0:wc0 + 128], rhs=xnT[:, kc, tg * 512:(tg + 1) * 512],
                        start=(kc == 0), stop=(kc == 15)), reads=[wk], writes=[kb])
                tsl = slice(tg * 512, (tg + 1) * 512)
                if kind == "q":
                    p.op("scalar", lambda e, pb=pb, tsl=tsl, s=s: e.copy(out=qT[s][:, tsl], in_=pb[:]), reads=[kb], writes=[("qT", s)])
                elif kind == "k":
                    p.op("scalar", lambda e, pb=pb, tg=tg, s=s: e.copy(out=kT[s][:, T + tg * 512:T + (tg + 1) * 512], in_=pb[:]),
                         reads=[kb], writes=[("kTo", s)])
                else:
                    p.op("scalar", lambda e, pb=pb, tsl=tsl, s=s: e.activation(out=gzT[s][:, tsl], in_=pb[:], func=AF.Silu),
                         reads=[kb], writes=[("gz", s)])
        for i4 in range(4):
            a = cnt["pa"] % 2
            cnt["pa"] += 1
            pb = PS[a]
            kb = "PS%d" % a
            for j in range(4):
                i = i4 * 4 + j
                for kc in range(16):
                    p.op("tensor", lambda e, pb=pb, kc=kc, i=i, j=j, w=w: e.matmul(
                        pb[:, j * 128:(j + 1) * 128], lhsT=xnT[:, kc, i * 128:(i + 1) * 128], rhs=w[:, kc, 256:384],
                        start=(kc == 0), stop=(kc == 15)), reads=[wk], writes=[kb])
            vs = cnt["v"] % 2
            cnt["v"] += 1
            p.op("vector", lambda e, pb=pb, vs=vs: e.tensor_copy(out=vst[vs][:], in_=pb[:].rearrange("p (a b) -> p a b", b=128)),
                 reads=[kb], writes=["vst%d" % vs])
            p.dma("sync", vscr[T + i4 * 512:T + (i4 + 1) * 512, h * 128:(h + 1) * 128].rearrange("(a p) f -> p a f", p=128),
                  vst[vs][:], reads=["vst%d" % vs], writes=[("vscr", h, i4)])
        for part in range(8):
            p.op("scalar", lambda e, part=part, s=s: e.activation(out=sq[:], in_=kT[s][:, part * 512:(part + 1) * 512], func=AF.Square),
                 reads=[("kTh", s), ("kTo", s)], writes=["sq"])
            p.op("tensor", lambda e: e.matmul(PS[2][0:1, :], lhsT=onesb[:, 0:1], rhs=sq[:], start=True, stop=True),
                 reads=["sq", "onesb"], writes=["PS2"])
            p.op("scalar", lambda e: e.copy(out=krow[:], in_=PS[2][0:1, :]), reads=["PS2"], writes=["krow"])
            p.op("vector", lambda e, part=part: e.tensor_reduce(out=km8[:, part:part + 1], in_=krow[:], axis=mybir.AxisListType.X, op=ALU.max),
                 reads=["krow"], writes=["km8"])
        p.op("vector", lambda e: e.tensor_reduce(out=kmax[:], in_=km8[:], axis=mybir.AxisListType.X, op=ALU.max), reads=["km8"], writes=["kmax"])
        for part in range(4):
            p.op("scalar", lambda e, part=part, s=s: e.activation(out=sq[:], in_=qT[s][:, part * 512:(part + 1) * 512], func=AF.Square),
                 reads=[("qT", s)], writes=["sq"])
            p.op("tensor", lambda e: e.matmul(PS[2][0:1, :], lhsT=onesb[:, 0:1], rhs=sq[:], start=True, stop=True),
                 reads=["sq", "onesb"], writes=["PS2"])
            p.op("vector", lambda e, part=part: e.tensor_scalar(out=qn2[:, part * 512:(part + 1) * 512], in0=PS[2][0:1, :], scalar1=kmax[:, 0:1],
                                                                scalar2=None, op0=ALU.mult), reads=["PS2", "kmax"], writes=["qn2"])
        p.op("scalar", lambda e: e.activation(out=qn2[:], in_=qn2[:], func=AF.Sqrt), reads=["qn2"], writes=["qn2"])
        p.op("vector", lambda e, s=s: e.tensor_scalar(out=negc[s][:], in0=qn2[:], scalar1=-1.0, scalar2=None, op0=ALU.mult),
             reads=["qn2"], writes=[("negc", s)])

    def stage_b(h):
        s = h % 2
        base = {1: 0, 4: 17, 16: 37}
        own = vscr[T:2 * T, :]
        hal = vscr[0:T, :]
        rk = [("vscr", h, i4) for i4 in range(4)]
        first = True
        for d in (1, 4, 16):
            nb = 16 // d
            b0 = base[d]
            vo = own.rearrange("(n j r) f -> j n r f", j=128, r=d)[:, :, :, h * 128:(h + 1) * 128]
            vh = hal.rearrange("(n j r) f -> j n r f", j=128, r=d)[:, nb - 1:nb, :, h * 128:(h + 1) * 128]
            for n_ in range(nb):
                p.dma("sync", v_h[:, b0 + d + n_ * d:b0 + d + (n_ + 1) * d, :], vo[:, n_, :, :], reads=rk, writes=["v_h" if (first and n_ == 0) else ("v_h", d, n_)])
                first = False
            p.dma("sync", v_h[:, b0:b0 + d, :], vh[:, 0, :, :], writes=[("v_hh", d)])
        it = 0
        for d in (1, 4, 16):
            nb = 16 // d
            b0 = base[d]
            for r in range(d):
                for n_ in range(nb):
                    a = it % 2
                    it += 1
                    ps_s = PS[4 + a]
                    ps_o = PS[6 + a]
                    ks, ko = "PS%d" % (4 + a), "PS%d" % (6 + a)
                    q0 = d * 128 * n_ + r
                    qsl = slice(q0, q0 + d * 127 + 1, d)
                    kc_sl = slice(T + q0, T + q0 + d * 127 + 1, d)
                    kp_sl = slice(T + q0 - 128 * d, T + q0 - 128 * d + d * 127 + 1, d)
                    tcur = b0 + (n_ + 1) * d + r
                    tprev = b0 + n_ * d + r
                    rd = [("qT", s), ("kTh", s), ("kTo", s), ("negc", s), "onesb"]
                    for half, ksl in ((0, kp_sl), (1, kc_sl)):
                        p.op("tensor", lambda e, ps_s=ps_s, half=half, ksl=ksl, qsl=qsl, s=s: e.matmul(
                            ps_s[:, half * 128:(half + 1) * 128], lhsT=kT[s][:, ksl], rhs=qT[s][:, qsl], start=True, stop=False),
                            reads=rd, writes=[ks])
                        p.op("tensor", lambda e, ps_s=ps_s, half=half, qsl=qsl, s=s: e.matmul(
                            ps_s[:, half * 128:(half + 1) * 128], lhsT=onesb[0:1, :], rhs=negc[s][:, qsl], start=False, stop=True),
                            reads=rd, writes=[ks])
                    p.op("scalar", lambda e, ps_s=ps_s, a=a: e.activation(out=Eb[a][:], in_=ps_s[:, 0:256], func=AF.Exp, scale=SCALE),
                         reads=[ks], writes=["Eb%d" % a])
                    if n_ == 0:
                        p.op("vector", lambda e, a=a: e.scalar_tensor_tensor(out=Pb[a][:, 0:128], in0=Eb[a][:, 0:128], scalar=hvB[:, 0:1],
                                                                             in1=mask2[:, 0:128], op0=ALU.mult, op1=ALU.mult),
                             reads=["Eb%d" % a, "hvB", "mask2"], writes=["Pb%d" % a])
                        p.op("vector", lambda e, a=a: e.tensor_tensor(out=Pb[a][:, 128:256], in0=Eb[a][:, 128:256], in1=mask2[:, 128:256], op=ALU.mult),
                             reads=["Eb%d" % a, "mask2"], writes=[("Pb2", a)])
                    else:
                        p.op("vector", lambda e, a=a: e.tensor_tensor(out=Pb[a][:], in0=Eb[a][:], in1=mask2[:], op=ALU.mult),
                             reads=["Eb%d" % a, "mask2"], writes=["Pb%d" % a, ("Pb2", a)])
                    vrd = ["v_h", ("v_h", d, n_), ("v_hh", d), "Pb%d" % a, ("Pb2", a)] + ([("v_h", d, n_ - 1)] if n_ > 0 else [])
                    p.op("tensor", lambda e, ps_o=ps_o, a=a, tprev=tprev: e.matmul(ps_o[:, 0:128], lhsT=v_h[:, tprev, :], rhs=Pb[a][:, 0:128],
                                                                                  start=True, stop=False), reads=vrd, writes=[ko])
                    p.op("tensor", lambda e, ps_o=ps_o, a=a, tcur=tcur: e.matmul(ps_o[:, 0:128], lhsT=v_h[:, tcur, :], rhs=Pb[a][:, 128:256],
                                                                                start=False, stop=True), reads=vrd, writes=[ko])
                    p.op("tensor", lambda e, ps_o=ps_o, a=a: e.matmul(ps_o[:, 128:256], lhsT=onesb[:], rhs=Pb[a][:, 0:128],
                                                                      start=True, stop=False), reads=vrd + ["onesb"], writes=[ko])
                    p.op("tensor", lambda e, ps_o=ps_o, a=a: e.matmul(ps_o[:, 128:256], lhsT=onesb[:], rhs=Pb[a][:, 128:256],
                                                                      start=False, stop=True), reads=vrd + ["onesb"], writes=[ko])
                    dst = ONd[:, :, qsl]
                    srcp = ps_o[:, 0:256].rearrange("p (a b) -> p a b", b=128)
                    if d == 1:
                        p.op("vector", lambda e, dst=dst, srcp=srcp: e.tensor_copy(out=dst, in_=srcp), reads=[ko], writes=["ONd"])
                    else:
                        p.op("vector", lambda e, dst=dst, srcp=srcp: e.tensor_tensor(out=dst, in0=srcp, in1=dst, op=ALU.add),
                             reads=[ko, "ONd"], writes=["ONd"])
        p.op("scalar", lambda e: e.activation(out=ONd[:, 1, :], in_=ONd[:, 1, :], func=AF.Ln), reads=["ONd"], writes=["ONd"])
        p.op("scalar", lambda e: e.activation(out=ONd[:, 1, :], in_=ONd[:, 1, :], func=AF.Exp, scale=-1.0), reads=["ONd"], writes=["ONd"])
        p.op("vector", lambda e: e.tensor_tensor(out=ONd[:, 0, :], in0=ONd[:, 0, :], in1=ONd[:, 1, :], op=ALU.mult), reads=["ONd"], writes=["ONd"])
        p.op("gpsimd", lambda e, s=s: e.tensor_tensor(out=ydT[:], in0=ONd[:, 0, :], in1=gzT[s][:], op=ALU.mult),
             reads=["ONd", ("gz", s)], writes=["ydT"])
        p.dma("sync", yT[2048 + h * 128:2048 + (h + 1) * 128, :], ydT[:], reads=["ydT"], writes=[])

    nheads = _STOP.get("heads", 16)
    stage_a(0)
    for h in range(nheads):
        if h + 1 < nheads:
            stage_a(h + 1)
        stage_b(h)
    c.pop()
    c.pop()
    if _STOP.get("l1") == "attn":
        p.emit()
        return c
    c.push()
    out_proj(c, yT, w_out, x1, x2, PS, False)
    c.pop()
    c.push()
    fgB = c.bc("fgB", final_g, D)
    xt = [c.sb("fxt%d" % s, [128, D], F32) for s in range(2)]
    fo = [c.sb("fo%d" % s, [128, D], F32) for s in range(2)]
    fjunk = c.sb("fjunk", [128, D], F32)
    fss = [c.sb("fss%d" % s, [128, 1], F32) for s in range(2)]
    frs = [c.sb("frs%d" % s, [128, 1], F32) for s in range(2)]
    for i in range(NT):
        s = i % 2
        p.dma("sync", xt[s][:], x2[i * 128:(i + 1) * 128, :], writes=["fxt%d" % s])
        p.op("scalar", lambda e, s=s: e.activation(out=fjunk[:], in_=xt[s][:], func=AF.Square), reads=["fxt%d" % s], writes=["fjunk"])
        p.op("vector", lambda e, s=s: e.tensor_reduce(out=fss[s][:], in_=fjunk[:], axis=mybir.AxisListType.X, op=ALU.add),
             reads=["fjunk"], writes=["fss%d" % s])
        p.op("vector", lambda e, s=s: e.tensor_scalar(out=fss[s][:], in0=fss[s][:], scalar1=1.0 / D, scalar2=EPS, op0=ALU.mult, op1=ALU.add),
             reads=["fss%d" % s], writes=["fss%d" % s])
        p.op("scalar", lambda e, s=s: e.activation(out=fss[s][:], in_=fss[s][:], func=AF.Sqrt), reads=["fss%d" % s], writes=["fss%d" % s])
        p.op("vector", lambda e, s=s: e.reciprocal(out=frs[s][:], in_=fss[s][:]), reads=["fss%d" % s], writes=["frs%d" % s])
        p.op("vector", lambda e, s=s: e.scalar_tensor_tensor(out=fo[s][:], in0=xt[s][:], scalar=frs[s][:], in1=fgB[:], op0=ALU.mult, op1=ALU.mult),
             reads=["fxt%d" % s, "frs%d" % s, "fgB"], writes=["fo%d" % s])
        p.dma("sync", out[i * 128:(i + 1) * 128, :], fo[s][:], reads=["fo%d" % s], writes=[], is_output=True)
    c.pop()
    p.emit()
    return c
```
